# Optimizing a Trainium2 kernel written in Bass

```python
import math
import jax, jax.numpy as jnp
from jax import lax
import numpy as np

D_MODEL = 1024
BATCH = 8
SEQ = 4096
DEPTH = 4

MIX_WIDTH = D_MODEL
N_MIXERS = 4
GROUP_WIDTH = MIX_WIDTH // N_MIXERS
ATT_HEADS = 4
ATT_VDIM = GROUP_WIDTH // ATT_HEADS
ATT_QK = ATT_VDIM // 2
Q_BLOCK = 128
CONV_WIDTH = 31
CONV_GROUPS = 4
FNET_GROUPS = 4
SGU_CHUNK = 128
SGU_GROUPS = 4
D_FF = 4 * D_MODEL
EPS = 1e-6
A_COLS = 3 * GROUP_WIDTH
B_COLS = 2 * GROUP_WIDTH
C_COLS = GROUP_WIDTH
D_COLS = 2 * GROUP_WIDTH
IN_COLS = A_COLS + B_COLS + C_COLS + D_COLS

kernel_name = "hybrid_parallel_mixer_encoder"


def _rmsnorm(x, g):
    xf = x.astype(jnp.float32)
    y = xf * lax.rsqrt(jnp.mean(xf * xf, axis=-1, keepdims=True) + EPS)
    return (y * g.astype(jnp.float32)).astype(x.dtype)


def _layernorm(x, g, b):
    xf = x.astype(jnp.float32)
    mu = jnp.mean(xf, axis=-1, keepdims=True)
    xc = xf - mu
    var = jnp.mean(xc * xc, axis=-1, keepdims=True)
    y = xc * lax.rsqrt(var + EPS) * g.astype(jnp.float32) + b.astype(jnp.float32)
    return y.astype(x.dtype)


def _alibi_slopes(n):
    return jnp.asarray([2.0 ** (-8.0 * (i + 1) / n) for i in range(n)], dtype=jnp.float32)


def _diff_attention(q, k, v, lam, lam_init, subln_g):
    bsz, seq = q.shape[0], q.shape[1]
    nb = seq // Q_BLOCK
    scale = ATT_QK ** -0.5
    slopes = _alibi_slopes(ATT_HEADS)
    kpos = jnp.arange(seq, dtype=jnp.float32)
    qb = q.reshape(bsz, nb, Q_BLOCK, ATT_HEADS, 2, ATT_QK).transpose(1, 0, 2, 3, 4, 5)

    def block(args):
        q_blk, i = args
        qpos = (i * Q_BLOCK + jnp.arange(Q_BLOCK)).astype(jnp.float32)
        bias = -slopes[:, None, None] * jnp.abs(qpos[:, None] - kpos[None, :])
        s = jnp.einsum('bqhmd,bkhmd->bhmqk', q_blk, k,
                       preferred_element_type=jnp.float32) * scale + bias[None, :, None]
        p = jax.nn.softmax(s, axis=-1)
        w = p[:, :, 0] - lam * p[:, :, 1]
        return jnp.einsum('bhqk,bkhe->bqhe', w.astype(v.dtype), v)

    o = lax.map(block, (qb, jnp.arange(nb)))
    o = o.transpose(1, 0, 2, 3, 4).reshape(bsz, seq, ATT_HEADS, ATT_VDIM)
    o = _rmsnorm(o, subln_g) * (1.0 - lam_init)
    return o.reshape(bsz, seq, GROUP_WIDTH)


def _conformer_conv(h, dw_w, dw_b, ln_g, ln_b, pw_w, pw_b):
    a, g = jnp.split(h, 2, axis=-1)
    z = a * jax.nn.sigmoid(g)
    z = lax.conv_general_dilated(
        z, dw_w[:, None, :], window_strides=(1,),
        padding=[(CONV_WIDTH // 2, CONV_WIDTH // 2)],
        dimension_numbers=('NWC', 'WIO', 'NWC'),
        feature_group_count=GROUP_WIDTH) + dw_b
    zg = z.reshape(z.shape[0], z.shape[1], CONV_GROUPS, GROUP_WIDTH // CONV_GROUPS)
    zg = _layernorm(zg, ln_g.reshape(CONV_GROUPS, -1), ln_b.reshape(CONV_GROUPS, -1))
    z = jax.nn.silu(zg.reshape(z.shape))
    return z @ pw_w + pw_b


def _fourier_mix(c, w, b):
    bsz, seq = c.shape[0], c.shape[1]
    cg = c.reshape(bsz, seq, FNET_GROUPS, GROUP_WIDTH // FNET_GROUPS).astype(jnp.float32)
    f = jnp.fft.fft2(cg, axes=(1, 3), norm='ortho').real.astype(c.dtype)
    return jnp.einsum('bsgc,gce->bsge', f, w).reshape(bsz, seq, GROUP_WIDTH) + b


def _spatial_gate(h, ln_g, ln_b, w_s, b_s):
    bsz, seq = h.shape[0], h.shape[1]
    u, v = jnp.split(h, 2, axis=-1)
    v = _layernorm(v, ln_g, ln_b)
    nc = seq // SGU_CHUNK
    vg = v.reshape(bsz, nc, SGU_CHUNK, SGU_GROUPS, GROUP_WIDTH // SGU_GROUPS)
    sv = jnp.einsum('gts,bnsgc->bntgc', w_s, vg) + b_s.T[None, None, :, :, None]
    return u * sv.reshape(bsz, seq, GROUP_WIDTH)


def setup_inputs(seed: int = 0) -> dict:
    key = jax.random.key(seed)
    ks = jax.random.split(key, 26)

    def nrm(k, shape, scale):
        return jax.random.normal(k, shape, jnp.float32) * scale

    def gain(k, shape):
        return 1.0 + 0.02 * jax.random.normal(k, shape, jnp.float32)

    gw = GROUP_WIDTH
    return {
        "x": nrm(ks[0], (BATCH, SEQ, D_MODEL), 1.0),
        "norm1_g": gain(ks[1], (DEPTH, D_MODEL)),
        "w_in": nrm(ks[2], (DEPTH, D_MODEL, IN_COLS), D_MODEL ** -0.5),
        "lam_q1": nrm(ks[3], (DEPTH, ATT_QK), 0.1),
        "lam_k1": nrm(ks[4], (DEPTH, ATT_QK), 0.1),
        "lam_q2": nrm(ks[5], (DEPTH, ATT_QK), 0.1),
        "lam_k2": nrm(ks[6], (DEPTH, ATT_QK), 0.1),
        "subln_g": gain(ks[7], (DEPTH, ATT_VDIM)),
        "conv_dw_w": nrm(ks[8], (DEPTH, CONV_WIDTH, gw), CONV_WIDTH ** -0.5),
        "conv_dw_b": nrm(ks[9], (DEPTH, gw), 0.02),
        "conv_ln_g": gain(ks[10], (DEPTH, gw)),
        "conv_ln_b": nrm(ks[11], (DEPTH, gw), 0.02),
        "conv_pw_w": nrm(ks[12], (DEPTH, gw, gw), gw ** -0.5),
        "conv_pw_b": nrm(ks[13], (DEPTH, gw), 0.02),
        "fnet_w": nrm(ks[14], (DEPTH, FNET_GROUPS, gw // FNET_GROUPS, gw // FNET_GROUPS),
                      (gw // FNET_GROUPS) ** -0.5),
        "fnet_b": nrm(ks[15], (DEPTH, gw), 0.02),
        "sgu_ln_g": gain(ks[16], (DEPTH, gw)),
        "sgu_ln_b": nrm(ks[17], (DEPTH, gw), 0.02),
        "sgu_w": nrm(ks[18], (DEPTH, SGU_GROUPS, SGU_CHUNK, SGU_CHUNK), SGU_CHUNK ** -0.5),
        "sgu_b": gain(ks[19], (DEPTH, SGU_GROUPS, SGU_CHUNK)),
        "w_out": nrm(ks[20], (DEPTH, MIX_WIDTH, D_MODEL), MIX_WIDTH ** -0.5),
        "norm2_g": gain(ks[21], (DEPTH, D_MODEL)),
        "w_up": nrm(ks[22], (DEPTH, D_MODEL, D_FF), D_MODEL ** -0.5),
        "w_down": nrm(ks[23], (DEPTH, D_FF, D_MODEL), D_FF ** -0.5),
        "final_g": gain(ks[24], (D_MODEL,)),
    }


def reference(x, norm1_g, w_in, lam_q1, lam_k1, lam_q2, lam_k2, subln_g,
              conv_dw_w, conv_dw_b, conv_ln_g, conv_ln_b, conv_pw_w, conv_pw_b,
              fnet_w, fnet_b, sgu_ln_g, sgu_ln_b, sgu_w, sgu_b,
              w_out, norm2_g, w_up, w_down, final_g):
    bsz, seq = x.shape[0], x.shape[1]
    splits = [GROUP_WIDTH, 2 * GROUP_WIDTH, A_COLS, A_COLS + B_COLS, A_COLS + B_COLS + C_COLS]
    for l in range(DEPTH):
        xn = _rmsnorm(x, norm1_g[l])
        h = xn @ w_in[l]
        hq, hk, hv, hb, hc, hd = jnp.split(h, splits, axis=-1)
        q = hq.reshape(bsz, seq, ATT_HEADS, 2, ATT_QK)
        k = hk.reshape(bsz, seq, ATT_HEADS, 2, ATT_QK)
        v = hv.reshape(bsz, seq, ATT_HEADS, ATT_VDIM)
        lam_init = 0.8 - 0.6 * math.exp(-0.3 * l)
        lam = (jnp.exp(jnp.sum((lam_q1[l] * lam_k1[l]).astype(jnp.float32)))
               - jnp.exp(jnp.sum((lam_q2[l] * lam_k2[l]).astype(jnp.float32)))
               + lam_init)
        y_a = _diff_attention(q, k, v, lam, lam_init, subln_g[l])
        y_b = _conformer_conv(hb, conv_dw_w[l], conv_dw_b[l], conv_ln_g[l], conv_ln_b[l],
                              conv_pw_w[l], conv_pw_b[l])
        y_c = _fourier_mix(hc, fnet_w[l], fnet_b[l])
        y_d = _spatial_gate(hd, sgu_ln_g[l], sgu_ln_b[l], sgu_w[l], sgu_b[l])
        y = jnp.concatenate([y_a, y_b, y_c, y_d], axis=-1)
        x = x + y @ w_out[l]
        xn2 = _rmsnorm(x, norm2_g[l])
        x = x + jnp.square(jax.nn.relu(xn2 @ w_up[l])) @ w_down[l]
    return _rmsnorm(x, final_g)
```

```python
import math
import contextlib
import numpy as np
import ml_dtypes
import concourse.bass as bass
import concourse.mybir as mybir
from concourse.bass_utils import run_bass_kernel_spmd

F32 = mybir.dt.float32
BF16 = mybir.dt.bfloat16
ALU = mybir.AluOpType
AF = mybir.ActivationFunctionType
AX = mybir.AxisListType

NT, D, NL, DFF = 4096, 1024, 4, 4096
EPS = 1e-6
BAND_LOGIT = 48.0
O_N1, O_N2, O_SLG, O_SLB, O_PWB, O_FNB, O_SUB, O_LAM, O_FIN = 0, 1024, 2048, 2304, 2560, 2816, 3072, 3136, 3264
NV = 3264 + 1024
NCV = 10


class Src:
    def __init__(self, name, sem, step):
        self.name, self.sem, self.step, self.count = name, sem, step, 0


class Buf:
    def __init__(self, name, dsrc=None):
        self.name, self.w, self.r, self.dsrc = name, None, [], dsrc


class T:
    def __init__(self, t, b):
        self.t, self.b = t, b

    def __getitem__(self, k):
        return self.t[k]


class Rec:
    def __init__(self):
        self.calls = []

    def __getattr__(self, name):
        def f(*a, **k):
            self.calls.append((name, a, k))
            return self
        return f


class Sched:
    def __init__(self, nc, sems):
        self.nc = nc
        self.eng = {}
        for nm in ("pe", "act", "dve", "pool", "sp"):
            self.eng[nm] = (Src(nm, sems.pop(), 1), {})
        self.prog = {nm: [] for nm in self.eng}
        allsrc = [Src("d%d" % i, s, 16) for i, s in enumerate(sems)]
        self.dfree_sw = allsrc[:12]
        self.dfree = allsrc[12:]
        self.dall = list(allsrc)
        self.n_inst = 0

    def _deps(self, me, reads, writes):
        deps = {}

        def add(t, war=False):
            if t is None:
                return
            s, v = t
            if s is me and me.name == "pe":
                return
            if deps.get(s, 0) < v:
                deps[s] = v
        for x in reads:
            add(x.b.w)
        for x in writes:
            add(x.b.w)
            for t in x.b.r:
                add(t, war=True)
        return deps

    def _wait(self, en, deps):
        me, waited = self.eng[en]
        for s, v in deps.items():
            if waited.get(s, 0) < v:
                self.prog[en].append(lambda hh, s=s, v=v: hh.wait_ge(s.sem, v))
                waited[s] = v
                self.n_inst += 1

    def _mark(self, t, reads, writes):
        for x in reads:
            x.b.r.append(t)
        for x in writes:
            x.b.w = t
            x.b.r = []

    @staticmethod
    def record(fns):
        if not isinstance(fns, (list, tuple)):
            fns = [fns]
        rec = Rec()
        for f in fns:
            f(rec)
        return rec.calls

    def op(self, en, fns, reads=(), writes=()):
        self.emit(en, self.record(fns), reads, writes)

    def emit(self, en, calls, reads=(), writes=()):
        me, _ = self.eng[en]
        self._wait(en, self._deps(me, reads, writes))
        me.count += 1
        for (n, a, k) in calls[:-1]:
            self.prog[en].append(lambda hh, n=n, a=a, k=k: getattr(hh, n)(*a, **k))
        n, a, k = calls[-1]
        self.prog[en].append(lambda hh, n=n, a=a, k=k, sem=me.sem: getattr(hh, n)(*a, **k).then_inc(sem, 1))
        self.n_inst += len(calls)
        self._mark((me, me.count), reads, writes)

    def dma(self, en, out, in_, reads=(), writes=()):
        d = None
        for x in list(writes) + list(reads):
            if x.b.dsrc is not None:
                d = x.b.dsrc
                break
        assert d is not None
        deps = self._deps(None, reads, writes)
        for x in writes:
            if x.b.w is not None and x.b.w[0] is d and deps.get(d, 0) == x.b.w[1]:
                del deps[d]
        self._wait(en, deps)
        d.count += 16
        self.prog[en].append(lambda hh, sem=d.sem: hh.dma_start(out=out, in_=in_).then_inc(sem, 16))
        self.n_inst += 1
        self._mark((d, d.count), reads, writes)

    def barrier(self):
        cur = {}
        for nm, (s, w) in self.eng.items():
            cur[s] = s.count
        for d in self.dall:
            cur[d] = d.count
        for nm in self.eng:
            self._wait(nm, {s: v for s, v in cur.items() if v > 0 and s is not self.eng[nm][0]})

    def flush(self, block):
        def mk(nm):
            def f(h):
                for c in self.prog[nm]:
                    c(h)
            return f
        block.tensor(mk("pe"))
        block.scalar(mk("act"))
        block.vector(mk("dve"))
        block.gpsimd(mk("pool"))
        block.sync(mk("sp"))


class Phase:
    def __init__(self, mk):
        self.mk, self.es, self.dsrcs, self.dsrcs_sw = mk, contextlib.ExitStack(), [], []

    def __enter__(self):
        self.es.__enter__()
        return self

    def __exit__(self, *a):
        self.mk.S.barrier()
        self.mk.S.dfree.extend(self.dsrcs)
        self.mk.S.dfree_sw.extend(self.dsrcs_sw)
        return self.es.__exit__(*a)

    def sb(self, shape, dt, dma=False):
        mk = self.mk
        mk.uid += 1
        t = self.es.enter_context(mk.nc.sbuf_tensor("t%d" % mk.uid, list(shape), dt))
        d = None
        if dma == "sw":
            d = mk.S.dfree_sw.pop()
            self.dsrcs_sw.append(d)
        elif dma:
            d = mk.S.dfree.pop()
            self.dsrcs.append(d)
        return T(t, Buf("t%d" % mk.uid, d))

    def ps(self, shape, dt=F32):
        mk = self.mk
        mk.uid += 1
        t = self.es.enter_context(mk.nc.psum_tensor("p%d" % mk.uid, list(shape), dt))
        return T(t, Buf("p%d" % mk.uid))


class Ring:
    def __init__(self, items):
        self.items, self.i = items, 0

    def next(self):
        x = self.items[self.i % len(self.items)]
        self.i += 1
        return x


def ev(i):
    return "act" if i % 2 == 0 else "dve"


class BG:
    def __init__(self, S):
        self.S, self.q = S, []

    def op(self, en, fns, reads=(), writes=()):
        calls = Sched.record(fns)
        self.q.append(lambda: self.S.emit(en, calls, reads, writes))

    def dma(self, *a, **k):
        self.q.append(lambda: self.S.dma(*a, **k))

    def pump(self, n):
        for _ in range(min(n, len(self.q))):
            self.q.pop(0)()

    def flush(self):
        self.pump(len(self.q))


def cp(en, out, in_, scale=None):
    if en == "act":
        if scale is None:
            return lambda h: h.activation(out=out, in_=in_, func=AF.Copy)
        return lambda h: h.activation(out=out, in_=in_, func=AF.Copy, scale=scale)
    if scale is None:
        return lambda h: h.tensor_copy(out=out, in_=in_)
    return lambda h: h.tensor_scalar_mul(out=out, in0=in_, scalar1=scale)


class MK:
    def __init__(self, nlayers=NL, final=True, dbg=False):
        self.nlayers, self.final, self.dbg = nlayers, final, dbg
        self.uid = 0
        nc = self.nc = bass.Bass("TRN2", target_bir_lowering=False)

        def din(name, shape, dt=F32):
            return nc.dram_tensor(name, list(shape), dt, kind="ExternalInput").ap()
        self.x = din("x", [NT, D])
        self.w_in = din("w_in", [NL, D, 2048])
        self.w_out = din("w_out", [NL, D, D])
        self.w_up = din("w_up", [NL, D, DFF])
        self.w_down = din("w_down", [NL, DFF, D])
        self.pvec = din("pvec", [NL, NV])
        self.colv = din("colv", [NL, 128, NCV])
        self.convdiag = din("convdiag", [NL, 128, 62, 128])
        self.sgu_wT = din("sgu_wT", [NL, 128, 4, 128])
        self.fnet_blk = din("fnet_blk", [NL, 128, 2, 128])
        self.pw_w = din("pw_w", [NL, 256, 256])
        self.c_identb = din("c_identb", [128, 128], BF16)
        self.c_identf = din("c_identf", [128, 128])
        self.c_aug = din("c_aug", [4, 12, NT], BF16)
        self.c_db = din("c_db", [128, 4, 128], BF16)
        self.c_tab = din("c_tab", [2, 16, 128, 2, 2, 1024], BF16)
        self.c_cc = din("c_cc", [128, 5, 128], BF16)
        self.c_g64 = din("c_g64", [128, 128])
        kind = "ExternalOutput" if dbg else "Internal"
        self.out = nc.dram_tensor("out", [NT, D], F32, kind="ExternalOutput").ap()
        self.X = nc.dram_tensor("Xs", [NT, D], F32, kind=kind).ap()
        self.FT = nc.dram_tensor("FTs", [768, NT], BF16, kind=kind).ap()
        self.TM = nc.dram_tensor("TMs", [NT, 516], BF16, kind=kind).ap()
        self.Y = nc.dram_tensor("Ys", [NT, D], BF16, kind=kind).ap()
        self.XN2T = nc.dram_tensor("XN2Ts", [D, NT], BF16, kind=kind).ap()

    def build(self):
        nc = self.nc
        with contextlib.ExitStack() as es:
            sems = [es.enter_context(nc.semaphore("s%d" % i)) for i in range(60)]
            block = es.enter_context(nc.Block())
            S = self.S = Sched(nc, sems)
            with Phase(self) as g:
                self.idb = g.sb([128, 128], BF16, dma=True)
                self.idf = g.sb([128, 128], F32, dma=True)
                self.eps = g.sb([128, 1], F32)
                self.one = g.sb([128, 1], F32)
                S.dma("sp", self.idb[:], self.c_identb, writes=[self.idb])
                S.dma("sp", self.idf[:], self.c_identf, writes=[self.idf])
                S.op("dve", lambda h: h.memset(self.eps[:], EPS), writes=[self.eps])
                S.op("dve", lambda h: h.memset(self.one[:], 1.0), writes=[self.one])
                for l in range(self.nlayers):
                    xsrc = self.x if l == 0 else self.X
                    self.p1(l, xsrc)
                    self.p2(l)
                    with Phase(self) as pwd:
                        wd = pwd.sb([128, 32, 1024], BF16, dma="sw")
                        self.p3(l, after_loads=lambda: self.load_wd(l, wd))
                        self.p4(l)
                        with Phase(self) as pwu:
                            wu = pwu.sb([128, 8, DFF], BF16, dma="sw")
                            self.p5(l, xsrc, after_loads=lambda: self.load_wu(l, wu))
                            self.p6(l, (l == self.nlayers - 1), wu, wd)
            S.flush(block)
        return nc

    def p1(self, l, xsrc):
        S = self.S
        with Phase(self) as ph:
            wing = [ph.sb([128, 8, 512], BF16, dma="sw") for _ in range(4)]
            for gq in range(4):
                for c2 in range(0, 8, 4):
                    S.dma("pool", wing[gq][:, c2:c2 + 4, :],
                          self.w_in[l, c2 * 128:(c2 + 4) * 128, gq * 512:(gq + 1) * 512].rearrange("(c p) n -> p c n", p=128),
                          writes=[wing[gq]])
            wst = ph.sb([128, 4, 128], BF16, dma="sw")
            S.dma("pool", wst[:], self.sgu_wT[l], writes=[wst])
            g1 = ph.sb([128, 1024], F32, dma=True)
            S.dma("sp", g1[:], self.pvec[l:l + 1, O_N1:O_N1 + 1024].partition_broadcast(128), writes=[g1])
            lng = ph.sb([128, 512], F32, dma=True)
            S.dma("sp", lng[:], self.pvec[l:l + 1, O_SLG:O_SLG + 512].partition_broadcast(128), writes=[lng])
            colv = ph.sb([128, NCV], F32, dma=True)
            S.dma("sp", colv[:], self.colv[l], writes=[colv])
            xbr = [ph.sb([128, 4, 1024], F32, dma=True) for _ in range(2)]
            xnr = [ph.sb([128, 4, 1024], BF16) for _ in range(2)]
            xntr = [ph.sb([128, 8, 512], BF16) for _ in range(2)]
            ssr = [ph.sb([128, 4], F32) for _ in range(2)]
            rsr = [ph.sb([128, 4], F32) for _ in range(2)]
            junk = ph.sb([128, 1024], BF16)
            fstr = [ph.sb([128, 6, 512], BF16, dma=True) for _ in range(2)]
            tar = [ph.sb([128, 4, 516], BF16, dma=True) for _ in range(2)]
            tbr = [ph.sb([128, 4, 256], BF16, dma=True) for _ in range(2)]
            tk1r = [ph.sb([128, 4, 512], F32) for _ in range(2)]
            efr = Ring([ph.sb([128, 512], F32) for _ in range(2)])
            afr = Ring([ph.sb([128, 512], F32) for _ in range(2)])
            st = ph.sb([128, 4, 2], F32)
            mean = ph.sb([128, 4], F32)
            msq = ph.sb([128, 4], F32)
            var = ph.sb([128, 4], F32)
            rs = ph.sb([128, 4], F32)
            vnf = Ring([ph.sb([128, 256], F32) for _ in range(2)])
            vnb = Ring([ph.sb([128, 256], BF16) for _ in range(8)])
            ptp = Ring([ph.ps([128, 8, 128], BF16) for _ in range(2)])
            pf = Ring([ph.ps([128, 512]) for _ in range(6)])
            psg = pf
            for ta in tar:
                S.op("pool", lambda h, ta=ta: h.memset(
                    ta[:, :, 0:260].rearrange("p j (h e) -> p j h e", e=65)[:, :, :, 64:65], 1.0), writes=[ta])
            xv = xsrc.rearrange("(b j p) d -> b p j d", j=4, p=128)

            bgA, bgB, bgC = BG(S), BG(S), BG(S)

            def xload(b):
                S.dma("sp", xbr[b % 2][:], xv[b], writes=[xbr[b % 2]])

            def prep(b):
                xb, xn, ss, rstd = xbr[b % 2], xnr[b % 2], ssr[b % 2], rsr[b % 2]
                for j in range(4):
                    bgB.op("act", lambda h, j=j: h.activation(out=junk[:], in_=xb[:, j, :], func=AF.Square,
                                                            accum_out=ss[:, j:j + 1]), reads=[xb], writes=[junk, ss])
                bgB.op("act", lambda h: h.activation(out=ss[:], in_=ss[:], func=AF.Ln, scale=1.0 / D, bias=self.eps[:]),
                     reads=[ss, self.eps], writes=[ss])
                bgB.op("act", lambda h: h.activation(out=rstd[:], in_=ss[:], func=AF.Exp, scale=-0.5), reads=[ss], writes=[rstd])
                for j in range(4):
                    bgB.op("dve", lambda h, j=j: h.scalar_tensor_tensor(
                        out=xn[:, j, :], in0=xb[:, j, :], scalar=rstd[:, j:j + 1], in1=g1[:],
                        op0=ALU.mult, op1=ALU.mult), reads=[xb, rstd, g1], writes=[xn])

            def transposes(b):
                xn, xnT = xnr[b % 2], xntr[b % 2]
                for j in range(4):
                    pt = ptp.next()
                    S.op("pe", [(lambda h, c=c: h.transpose(out=pt[:, c, :], in_=xn[:, j, c * 128:(c + 1) * 128],
                                                            identity=self.idb[:])) for c in range(8)],
                         reads=[xn, self.idb], writes=[pt])
                    S.op(ev(j), cp(ev(j), xnT[:, :, j * 128:(j + 1) * 128], pt[:]), reads=[pt], writes=[xnT])

            def feat(b):
                xnT, fst = xntr[b % 2], fstr[b % 2]

                def fchunk(col0):
                    p = pf.next()
                    wg, cl = wing[col0 // 512], col0 % 512
                    S.op("pe", [(lambda h, c=c: h.matmul(p[:], lhsT=wg[:, c, cl:cl + 128], rhs=xnT[:, c, :],
                                                         start=(c == 0), stop=(c == 7))) for c in range(8)],
                         reads=[wg, xnT], writes=[p])
                    return p
                for m in range(4):
                    p = fchunk(m * 128)
                    sc = (32.0 ** -0.5) if m < 2 else None
                    S.op(ev(m), cp(ev(m), fst[:, m, :], p[:], sc), reads=[p], writes=[fst])
                    bgA.pump(1)
                    bgB.pump(3)
                for c in range(2):
                    pg = fchunk(768 + c * 128)
                    ef, af = efr.next(), afr.next()
                    S.op("act", lambda h: h.activation(out=ef[:], in_=pg[:], func=AF.Exp, scale=-1.0), reads=[pg], writes=[ef])
                    bgA.pump(1)
                    bgB.pump(1)
                    bgC.op("act", lambda h: h.activation(out=ef[:], in_=ef[:], func=AF.Ln, bias=self.one[:]), reads=[ef, self.one], writes=[ef])
                    bgC.op("act", lambda h: h.activation(out=ef[:], in_=ef[:], func=AF.Exp, scale=-1.0), reads=[ef], writes=[ef])
                    pa = fchunk(512 + c * 128)
                    S.op("dve", cp("dve", af[:], pa[:]), reads=[pa], writes=[af])
                    bgC.op("pool", lambda h, c=c: h.tensor_tensor(out=fst[:, 4 + c, :], in0=af[:], in1=ef[:], op=ALU.mult),
                           reads=[af, ef], writes=[fst])
                    bgC.pump(2)
                    bgB.pump(2)
                bgC.flush()
                S.dma("sp", self.FT.rearrange("(m p) t -> p m t", p=128)[:, :, b * 512:(b + 1) * 512], fst[:], reads=[fst])

            def tok(b):
                xnT, ta, tk1 = xntr[b % 2], tar[b % 2], tk1r[b % 2]
                for j in range(4):
                    p0 = pf.next()
                    S.op("pe", [(lambda h, c=c: h.matmul(p0[:], lhsT=xnT[:, c, j * 128:(j + 1) * 128], rhs=wing[2][:, c, :],
                                                         start=(c == 0), stop=(c == 7))) for c in range(8)],
                         reads=[wing[2], xnT], writes=[p0])
                    S.op("act", cp("act", ta[:, j, 0:260].rearrange("p (h e) -> p h e", e=65)[:, :, 0:64],
                                   p0[:, 0:256].rearrange("p (h e) -> p h e", e=64)), reads=[p0], writes=[ta])
                    S.op("dve", cp("dve", ta[:, j, 260:516], p0[:, 256:512]), reads=[p0], writes=[ta])
                    bgA.pump(3)
                    p1 = pf.next()
                    S.op("pe", [(lambda h, c=c: h.matmul(p1[:], lhsT=xnT[:, c, j * 128:(j + 1) * 128], rhs=wing[3][:, c, :],
                                                         start=(c == 0), stop=(c == 7))) for c in range(8)],
                         reads=[wing[3], xnT], writes=[p1])
                    S.op(ev(j), cp(ev(j), tk1[:, j, :], p1[:]), reads=[p1], writes=[tk1])
                    bgA.pump(3)
                S.dma("sp", self.TM.rearrange("(b j p) n -> b p j n", j=4, p=128)[b], ta[:], reads=[ta])

            vbs = {}

            def sgu_pre(b):
                tk1 = tk1r[b % 2]
                for j in range(4):
                    bgA.op("act", lambda h, j=j: h.activation(out=junk[:, 0:256], in_=tk1[:, j, 256:512], func=AF.Copy,
                                                            accum_out=st[:, j, 0:1]), reads=[tk1], writes=[junk, st])
                    bgA.op("act", lambda h, j=j: h.activation(out=junk[:, 0:256], in_=tk1[:, j, 256:512], func=AF.Square,
                                                            accum_out=st[:, j, 1:2]), reads=[tk1], writes=[junk, st])
                bgA.op("dve", lambda h: h.tensor_scalar_mul(out=mean[:], in0=st[:, :, 0], scalar1=1.0 / 256), reads=[st], writes=[mean])
                bgA.op("dve", lambda h: h.tensor_tensor(out=msq[:], in0=mean[:], in1=mean[:], op=ALU.mult), reads=[mean], writes=[msq])
                bgA.op("dve", lambda h: h.scalar_tensor_tensor(out=var[:], in0=st[:, :, 1], scalar=1.0 / 256, in1=msq[:],
                                                             op0=ALU.mult, op1=ALU.subtract), reads=[st, msq], writes=[var])
                bgA.op("act", lambda h: h.activation(out=var[:], in_=var[:], func=AF.Ln, bias=self.eps[:]), reads=[var, self.eps], writes=[var])
                bgA.op("act", lambda h: h.activation(out=rs[:], in_=var[:], func=AF.Exp, scale=-0.5), reads=[var], writes=[rs])
                for j in range(4):
                    vf, vb = vnf.next(), vnb.next()
                    vbs[(b, j)] = vb
                    bgA.op("dve", lambda h, j=j: h.tensor_scalar(out=vf[:], in0=tk1[:, j, 256:512], scalar1=mean[:, j:j + 1],
                                                               scalar2=rs[:, j:j + 1], op0=ALU.subtract, op1=ALU.mult),
                         reads=[tk1, mean, rs], writes=[vf])
                    bgA.op("pool", lambda h: h.tensor_tensor(out=vf[:], in0=vf[:], in1=lng[:, 0:256], op=ALU.mult), reads=[vf, lng], writes=[vf])
                    bgA.op("pool", lambda h: h.tensor_tensor(out=vb[:], in0=vf[:], in1=lng[:, 256:512], op=ALU.add), reads=[vf, lng], writes=[vb])

            def sgu_mm(b):
                tk1, tb = tk1r[b % 2], tbr[b % 2]
                for j in range(4):
                    vb, pg = vbs.pop((b, j)), psg.next()
                    S.op("pe", [(lambda h, g=g: h.matmul(pg[:, g * 64:(g + 1) * 64], lhsT=wst[:, g, :], rhs=vb[:, g * 64:(g + 1) * 64],
                                                         start=True, stop=True)) for g in range(4)], reads=[wst, vb], writes=[pg])
                    for g in range(4):
                        S.op("dve", lambda h, j=j, g=g: h.scalar_tensor_tensor(
                            out=tb[:, j, g * 64:(g + 1) * 64], in0=pg[:, g * 64:(g + 1) * 64], scalar=colv[:, 6 + g:7 + g],
                            in1=tk1[:, j, g * 64:(g + 1) * 64], op0=ALU.add, op1=ALU.mult), reads=[pg, colv, tk1], writes=[tb])
                S.dma("sp", self.Y.rearrange("(b j p) n -> b p j n", j=4, p=128)[b][:, :, 768:1024], tb[:], reads=[tb])

            xload(0)
            xload(1)
            prep(0)
            bgB.flush()
            transposes(0)
            prep(1)
            for b in range(8):
                if b + 2 < 8:
                    xload(b + 2)
                feat(b)
                if b + 1 < 8:
                    bgB.flush()
                    transposes(b + 1)
                tok(b)
                if b > 0:
                    bgA.flush()
                    sgu_mm(b - 1)
                sgu_pre(b)
                if b + 2 < 8:
                    prep(b + 2)
            bgA.flush()
            sgu_mm(7)

    def p2(self, l):
        S = self.S
        lam_init = 0.8 - 0.6 * math.exp(-0.3 * l)
        with Phase(self) as ph:
            qar = [[ph.sb([128, NT], BF16, dma=True) for _ in range(2)] for _ in range(2)]
            kpr = [ph.sb([128, NT], BF16, dma=True) for _ in range(2)]
            knr = [ph.sb([128, NT], BF16, dma=True) for _ in range(2)]
            order = [qar[0][0], qar[0][1], kpr[0], knr[0], qar[1][0], qar[1][1], kpr[1], knr[1]]
            engs = ["dve", "pool", "act", "dve", "pool", "act", "dve", "pool"]
            for tz, en in zip(order, engs):
                if en == "act":
                    S.op("act", lambda hh, tz=tz: hh.memzero(tz[:]), writes=[tz])
                else:
                    S.op(en, lambda hh, tz=tz: hh.memset(tz[:], 0.0), writes=[tz])
            vp = ph.sb([128, 32, 260], BF16, dma=True)
            tmv = self.TM.rearrange("(k p) n -> p k n", p=128)
            for k4 in range(4):
                S.dma("sp", vp[:, k4 * 8:(k4 + 1) * 8, :], tmv[:, k4 * 8:(k4 + 1) * 8, 0:260], writes=[vp])
            db = ph.sb([128, 4, 128], BF16, dma=True)
            S.dma("sp", db[:], self.c_db, writes=[db])
            early_head0 = True
            subg = ph.sb([128, 64], F32, dma=True)
            S.dma("sp", subg[:], self.pvec[l:l + 1, O_SUB:O_SUB + 64].partition_broadcast(128), writes=[subg])
            S.op("dve", lambda h: h.tensor_scalar_mul(out=subg[:], in0=subg[:], scalar1=1.0 - lam_init), reads=[subg], writes=[subg])
            lq = ph.sb([128, 128], F32, dma=True)
            S.dma("sp", lq[:], self.pvec[l:l + 1, O_LAM:O_LAM + 128].partition_broadcast(128), writes=[lq])
            lp = ph.sb([128, 2, 32], F32)
            ls = ph.sb([128, 2], F32)
            nlam = ph.sb([128, 1], F32)
            S.op("dve", lambda h: h.tensor_tensor(out=lp[:, 0, :], in0=lq[:, 0:32], in1=lq[:, 32:64], op=ALU.mult), reads=[lq], writes=[lp])
            S.op("dve", lambda h: h.tensor_tensor(out=lp[:, 1, :], in0=lq[:, 64:96], in1=lq[:, 96:128], op=ALU.mult), reads=[lq, lp], writes=[lp])
            S.op("dve", lambda h: h.reduce_sum(out=ls[:], in_=lp[:], axis=AX.X), reads=[lp], writes=[ls])
            S.op("act", lambda h: h.activation(out=ls[:], in_=ls[:], func=AF.Exp), reads=[ls], writes=[ls])
            S.op("dve", lambda h: h.tensor_tensor(out=nlam[:], in0=ls[:, 1:2], in1=ls[:, 0:1], op=ALU.subtract), reads=[ls], writes=[nlam])
            S.op("dve", lambda h: h.tensor_scalar_add(out=nlam[:], in0=nlam[:], scalar1=-lam_init), reads=[nlam], writes=[nlam])
            ya = ph.sb([128, 32, 256], BF16, dma=True)
            ptr = Ring([ph.sb([128, 1024], BF16) for _ in range(4)])
            ot = ph.sb([128, 2, 512], F32)
            rz = ph.sb([128, 2, 4], F32)
            o4r = Ring([ph.sb([128, 4, 64], F32) for _ in range(2)])
            sq4 = ph.sb([128, 4, 64], F32)
            ss4r = Ring([ph.sb([128, 4], F32) for _ in range(2)])
            rs4r = Ring([ph.sb([128, 4], F32) for _ in range(2)])
            spr = Ring([ph.ps([128, 1024]) for _ in range(3)])
            oacc = [ph.ps([128, 512]) for _ in range(2)]

            def load_head(h):
                kp, kn = kpr[h % 2], knr[h % 2]
                for m in range(2):
                    qa = qar[h % 2][m]
                    r0 = h * 64 + m * 32
                    S.dma("sp", qa[64 * m:64 * m + 32, :], self.FT[r0:r0 + 32, :], writes=[qa])
                    S.dma("sp", qa[64 * m + 32:64 * m + 36, :], self.c_aug[h, 0:4, :], writes=[qa])
                    S.dma("sp", kp[64 * m:64 * m + 32, :], self.FT[256 + r0:256 + r0 + 32, :], writes=[kp])
                    S.dma("sp", kp[64 * m + 32:64 * m + 36, :], self.c_aug[h, 4:8, :], writes=[kp])
                    S.dma("sp", kn[64 * m:64 * m + 32, :], self.FT[256 + r0:256 + r0 + 32, :], writes=[kn])
                    S.dma("sp", kn[64 * m + 32:64 * m + 36, :], self.c_aug[h, 8:12, :], writes=[kn])

            deferred = []
            pend_pv = []

            def qk_step(h, qc, kt, first, last):
                kp, kn = kpr[h % 2], knr[h % 2]
                sp = spr.next()
                q0, k0 = qc * 512, kt * 128
                mms = []
                for m in range(2):
                    qa = qar[h % 2][m]
                    c0 = m * 512
                    if k0 + 128 <= q0 or k0 >= q0 + 512:
                        ka = kp if k0 + 128 <= q0 else kn
                        mms.append(lambda hh, ka=ka, qa=qa, c0=c0: hh.matmul(
                            sp[:, c0:c0 + 512], lhsT=ka[:, k0:k0 + 128], rhs=qa[:, q0:q0 + 512], start=True, stop=True))
                    else:
                        d = (k0 - q0) // 128
                        if d > 0:
                            mms.append(lambda hh, qa=qa, c0=c0: hh.matmul(
                                sp[:, c0:c0 + d * 128], lhsT=kn[:, k0:k0 + 128], rhs=qa[:, q0:q0 + d * 128], start=True, stop=True))
                        mms.append(lambda hh, qa=qa, c0=c0: hh.matmul(
                            sp[:, c0 + d * 128:c0 + 512], lhsT=kp[:, k0:k0 + 128],
                            rhs=qa[:, q0 + d * 128:q0 + 512], start=True, stop=False))
                        mms.append(lambda hh, c0=c0: hh.matmul(
                            sp[:, c0 + d * 128:c0 + (d + 1) * 128], lhsT=self.idb[:], rhs=db[:, h, :], start=False, stop=True))
                S.op("pe", mms, reads=[qar[h % 2][0], qar[h % 2][1], kp, kn, db, self.idb], writes=[sp])
                pt = ptr.next()
                S.op("act", lambda hh: hh.activation(out=pt[:], in_=sp[:], func=AF.Exp), reads=[sp], writes=[pt])
                pend_pv.append((h, qc, kt, pt, first, last))

            def pv_step(h, qc, kt, pt, first, last):
                S.op("pe", [(lambda hh, m=m: hh.matmul(oacc[m][0:65, :], lhsT=vp[:, kt, h * 65:(h + 1) * 65],
                                                       rhs=pt[:, m * 512:(m + 1) * 512], start=first, stop=last))
                            for m in range(2)], reads=[vp, pt], writes=[oacc[0], oacc[1]])
                if last:
                    post(h, qc)

            def post(h, qc):
                for m in range(2):
                    S.op("dve", cp("dve", ot[0:65, m, :], oacc[m][0:65, :]), reads=[oacc[m]], writes=[ot])
                tpb = spr.next()
                tp = [T(tpb.t[:, m * 512:(m + 1) * 512].rearrange("p (s e) -> p s e", e=128), tpb.b) for m in range(2)]
                S.op("pe", [(lambda hh, m=m, sb=sb: hh.transpose(out=tp[m][:, sb, 0:65], in_=ot[0:65, m, sb * 128:(sb + 1) * 128],
                                                                identity=self.idf[0:65, 0:65])) for m in range(2) for sb in range(4)],
                     reads=[ot, self.idf], writes=[tpb])
                o4, ss4, rs4 = o4r.next(), ss4r.next(), rs4r.next()
                for m in range(2):
                    S.op("dve", lambda hh, m=m: hh.reciprocal(out=rz[:, m, :], in_=tp[m][:, :, 64]), reads=[tp[m]], writes=[rz])
                S.op("dve", lambda hh: hh.tensor_scalar_mul(out=rz[:, 1, :], in0=rz[:, 1, :], scalar1=nlam[:, 0:1]), reads=[rz, nlam], writes=[rz])
                for sb in range(4):
                    S.op("dve", lambda hh, sb=sb: hh.tensor_scalar_mul(out=o4[:, sb, :], in0=tp[0][:, sb, 0:64], scalar1=rz[:, 0, sb:sb + 1]),
                         reads=[tp[0], rz], writes=[o4])
                    S.op("dve", lambda hh, sb=sb: hh.scalar_tensor_tensor(out=o4[:, sb, :], in0=tp[1][:, sb, 0:64], scalar=rz[:, 1, sb:sb + 1],
                                                                          in1=o4[:, sb, :], op0=ALU.mult, op1=ALU.add),
                         reads=[tp[1], rz, o4], writes=[o4])
                S.op("pool", lambda hh: hh.tensor_tensor(out=sq4[:], in0=o4[:], in1=o4[:], op=ALU.mult), reads=[o4], writes=[sq4])
                S.op("dve", lambda hh: hh.reduce_sum(out=ss4[:], in_=sq4[:], axis=AX.X), reads=[sq4], writes=[ss4])

                def fin():
                    S.op("act", lambda hh: hh.activation(out=ss4[:], in_=ss4[:], func=AF.Ln, scale=1.0 / 64, bias=self.eps[:]),
                         reads=[ss4, self.eps], writes=[ss4])
                    S.op("act", lambda hh: hh.activation(out=rs4[:], in_=ss4[:], func=AF.Exp, scale=-0.5), reads=[ss4], writes=[rs4])
                    for sb in range(4):
                        S.op("dve", lambda hh, sb=sb: hh.scalar_tensor_tensor(
                            out=ya[:, qc * 4 + sb, h * 64:(h + 1) * 64], in0=o4[:, sb, :], scalar=rs4[:, sb:sb + 1], in1=subg[:],
                            op0=ALU.mult, op1=ALU.mult), reads=[o4, rs4, subg], writes=[ya])
                deferred.append([6, fin])

            def tick():
                for d in list(deferred):
                    d[0] -= 1
                    if d[0] <= 0:
                        deferred.remove(d)
                        d[1]()

            load_head(0)
            for h in range(4):
                if h + 1 < 4:
                    load_head(h + 1)
                slope = 2.0 ** (-8.0 * (h + 1) / 4)
                dmax = BAND_LOGIT / slope
                for qc in range(8):
                    kts = [kt for kt in range(32)
                           if max(0, kt * 128 - (qc * 512 + 511), qc * 512 - (kt * 128 + 127)) < dmax]
                    for kt in kts:
                        qk_step(h, qc, kt, kt == kts[0], kt == kts[-1])
                        if len(pend_pv) > 2:
                            pv_step(*pend_pv.pop(0))
                        tick()
            while pend_pv:
                pv_step(*pend_pv.pop(0))
            while deferred:
                tick()
            yv = self.Y.rearrange("(k p) n -> p k n", p=128)
            for k4 in range(4):
                S.dma("sp", yv[:, k4 * 8:(k4 + 1) * 8, 0:256], ya[:, k4 * 8:(k4 + 1) * 8, :], reads=[ya])

    def p3(self, l, after_loads=None):
        S = self.S
        with Phase(self) as ph:
            zp = ph.sb([128, 2, NT + 30], BF16, dma=True)
            S.op("pool", lambda h: h.memset(zp[:, :, 0:15], 0.0), writes=[zp])
            S.op("pool", lambda h: h.memset(zp[:, :, NT + 15:NT + 30], 0.0), writes=[zp])
            for c in range(2):
                S.dma("sp", zp[:, c, 15:NT + 15], self.FT[512 + c * 128:512 + (c + 1) * 128, :], writes=[zp])
            dg = ph.sb([128, 62, 128], BF16, dma="sw")
            S.dma("pool", dg[:], self.convdiag[l], writes=[dg])
            pw = ph.sb([128, 2, 256], BF16, dma="sw")
            S.dma("pool", pw[:], self.pw_w[l].rearrange("(c p) n -> p c n", p=128), writes=[pw])
            g64 = ph.sb([128, 128], F32, dma=True)
            S.dma("sp", g64[:], self.c_g64, writes=[g64])
            colv = ph.sb([128, NCV], F32, dma=True)
            S.dma("sp", colv[:], self.colv[l], writes=[colv])
            pwb = ph.sb([128, 256], F32, dma=True)
            S.dma("sp", pwb[:], self.pvec[l:l + 1, O_PWB:O_PWB + 256].partition_broadcast(128), writes=[pwb])
            cvr = Ring([ph.sb([128, 512], F32) for _ in range(4)])
            sqr = Ring([ph.sb([128, 512], F32) for _ in range(4)])
            m2r = Ring([ph.sb([128, 512], F32) for _ in range(2)])
            vr = Ring([ph.sb([128, 512], F32) for _ in range(2)])
            dr = Ring([ph.sb([128, 512], F32) for _ in range(2)])
            er = Ring([ph.sb([128, 512], F32) for _ in range(2)])
            str_ = [ph.sb([128, 2, 512], BF16) for _ in range(2)]
            ybr = [ph.sb([128, 4, 256], BF16, dma=True) for _ in range(2)]
            pcv = Ring([ph.ps([128, 512]) for _ in range(2)])
            pmean = Ring([ph.ps([128, 512]) for _ in range(2)])
            pmsq = Ring([ph.ps([128, 512]) for _ in range(2)])
            ppw = Ring([ph.ps([128, 4, 256]) for _ in range(1)])
            wbg = after_loads() if after_loads is not None else BG(S)
            cvs = {}

            def stA(tb, c):
                pc, cv, sq = pcv.next(), cvr.next(), sqr.next()
                cvs[(tb, c)] = (cv, sq)
                S.op("pe", [(lambda h, j=j: h.matmul(pc[:], lhsT=dg[:, c * 31 + j, :], rhs=zp[:, c, tb * 512 + j:tb * 512 + j + 512],
                                                     start=(j == 0), stop=(j == 30))) for j in range(31)], reads=[dg, zp], writes=[pc])
                S.op("act", lambda h: h.activation(out=cv[:], in_=pc[:], func=AF.Identity, bias=colv[:, c:c + 1]), reads=[pc, colv], writes=[cv])
                S.op("pool", lambda h: h.tensor_tensor(out=sq[:], in0=cv[:], in1=cv[:], op=ALU.mult), reads=[cv], writes=[sq])

            def stB(tb, c):
                sT = str_[tb % 2]
                cv, sq = cvs.pop((tb, c))
                m2, v, d, e = m2r.next(), vr.next(), dr.next(), er.next()
                pm, pq = pmean.next(), pmsq.next()
                S.op("pe", lambda h: h.matmul(pm[:], lhsT=g64[:], rhs=cv[:], start=True, stop=True), reads=[g64, cv], writes=[pm])
                yield
                S.op("pe", lambda h: h.matmul(pq[:], lhsT=g64[:], rhs=sq[:], start=True, stop=True), reads=[g64, sq], writes=[pq])
                yield
                S.op("act", lambda h: h.activation(out=m2[:], in_=pm[:], func=AF.Square), reads=[pm], writes=[m2])
                yield
                S.op("dve", lambda h: h.tensor_tensor(out=v[:], in0=pq[:], in1=m2[:], op=ALU.subtract), reads=[pq, m2], writes=[v])
                yield
                S.op("act", lambda h: h.activation(out=v[:], in_=v[:], func=AF.Ln, bias=self.eps[:]), reads=[v, self.eps], writes=[v])
                yield
                S.op("act", lambda h: h.activation(out=v[:], in_=v[:], func=AF.Exp, scale=-0.5), reads=[v], writes=[v])
                yield
                S.op("dve", lambda h: h.tensor_tensor(out=d[:], in0=cv[:], in1=pm[:], op=ALU.subtract), reads=[cv, pm], writes=[d])
                yield
                S.op("pool", lambda h: h.tensor_tensor(out=d[:], in0=d[:], in1=v[:], op=ALU.mult), reads=[d, v], writes=[d])
                yield
                S.op("dve", lambda h: h.tensor_scalar(out=d[:], in0=d[:], scalar1=colv[:, 2 + c:3 + c], scalar2=colv[:, 4 + c:5 + c],
                                                      op0=ALU.mult, op1=ALU.add), reads=[d, colv], writes=[d])
                yield
                S.op("act", lambda h: h.activation(out=e[:], in_=d[:], func=AF.Exp, scale=-1.0), reads=[d], writes=[e])
                yield
                S.op("act", lambda h: h.activation(out=e[:], in_=e[:], func=AF.Ln, bias=self.one[:]), reads=[e, self.one], writes=[e])
                yield
                S.op("act", lambda h: h.activation(out=e[:], in_=e[:], func=AF.Exp, scale=-1.0), reads=[e], writes=[e])
                yield
                S.op("dve", lambda h: h.tensor_tensor(out=sT[:, c, :], in0=d[:], in1=e[:], op=ALU.mult), reads=[d, e], writes=[sT])
                yield

            def stC(tb):
                sT, yb = str_[tb % 2], ybr[tb % 2]
                pp = ppw.next()
                S.op("pe", [(lambda h, j=j, c=c: h.matmul(pp[:, j, :], lhsT=sT[:, c, j * 128:(j + 1) * 128], rhs=pw[:, c, :],
                                                          start=(c == 0), stop=(c == 1))) for j in range(4) for c in range(2)],
                     reads=[sT, pw], writes=[pp])
                for j in range(4):
                    S.op("dve", lambda h, j=j: h.tensor_tensor(out=yb[:, j, :], in0=pp[:, j, :], in1=pwb[:], op=ALU.add), reads=[pp, pwb], writes=[yb])
                S.dma("sp", self.Y.rearrange("(b j p) n -> b p j n", j=4, p=128)[tb][:, :, 256:512], yb[:], reads=[yb])

            stA(0, 0)
            stA(0, 1)
            for tb in range(8):
                if tb + 1 < 8:
                    stA(tb + 1, 0)
                    stA(tb + 1, 1)
                wbg.pump(2)
                g0, g1 = stB(tb, 0), stB(tb, 1)
                for _ in g0:
                    next(g1, None)
                for _ in g1:
                    pass
                if tb > 0:
                    stC(tb - 1)
            stC(7)
            wbg.flush()

    def p4(self, l):
        S = self.S
        H = NT // 2
        with Phase(self) as ph:
            ct = ph.sb([128, 32, 256], BF16, dma=True)
            tmv = self.TM.rearrange("(k p) n -> p k n", p=128)
            for k4 in range(4):
                S.dma("sp", ct[:, k4 * 8:(k4 + 1) * 8, :], tmv[:, k4 * 8:(k4 + 1) * 8, 260:516], writes=[ct])
            ccb = ph.sb([128, 5, 128], BF16, dma=True)
            S.dma("sp", ccb[:], self.c_cc, writes=[ccb])
            wf = ph.sb([128, 2, 128], BF16, dma="sw")
            S.dma("pool", wf[:], self.fnet_blk[l], writes=[wf])
            fnb = ph.sb([128, 256], F32, dma=True)
            S.dma("sp", fnb[:], self.pvec[l:l + 1, O_FNB:O_FNB + 256].partition_broadcast(128), writes=[fnb])
            tblr = Ring([ph.sb([128, 2, 2, 1024], BF16, dma=True) for _ in range(4)])
            y12 = ph.sb([128, 2, 2, H + 2], BF16)
            S.op("pool", lambda h: h.memset(y12[:, 1, :, H:H + 2], 0.0), writes=[y12])
            ftr = Ring([ph.sb([128, 2, 512], BF16) for _ in range(2)])
            orr = Ring([ph.sb([128, 256], BF16) for _ in range(4)])
            ycr = [ph.sb([128, 4, 256], BF16, dma=True) for _ in range(2)]
            acc = [ph.ps([128, 512]) for _ in range(8)]
            k = 0
            for pas in range(2):
                for kt in range(32):
                    if kt % 2 == 0:
                        tb2 = tblr.next()
                        ndma = getattr(self, "_ndma", 0)
                        self._ndma = ndma + 1
                        S.dma("sp" if ndma % 2 == 0 else "act", tb2[:], self.c_tab[pas, kt // 2], writes=[tb2])
                    tb = T(tb2.t[:, kt % 2], tb2.b)
                    for t in range(2):
                        for cc in range(2):
                            for n in range(2):
                                a = acc[t * 4 + cc * 2 + n]
                                S.op("pe", lambda h, t=t, cc=cc, n=n, a=a, tb=tb, kt=kt: h.matmul(
                                    a[:], lhsT=ct[:, kt, cc * 128:(cc + 1) * 128], rhs=tb[:, t, n * 512:(n + 1) * 512],
                                    start=(kt == 0), stop=(kt == 31)), reads=[ct, tb], writes=[a])
                for t in range(2):
                    for cc in range(2):
                        for n in range(2):
                            a = acc[t * 4 + cc * 2 + n]
                            s0 = pas * 1024 + n * 512
                            k += 1
                            S.op(ev(k), cp(ev(k), y12[:, t, cc, s0:s0 + 512], a[:]), reads=[a], writes=[y12])
            for cc in range(2):
                a = acc[cc]
                S.op("pe", [(lambda h, kt=kt: h.matmul(a[:, 0:1], lhsT=ct[:, kt, cc * 128:(cc + 1) * 128], rhs=ccb[:, 4, 0:1],
                                                       start=(kt == 0), stop=(kt == 31))) for kt in range(32)], reads=[ct, ccb], writes=[a])
                S.op("dve", cp("dve", y12[:, 0, cc, H:H + 1], a[:, 0:1]), reads=[a], writes=[y12])
            pF = Ring([acc[2], acc[3]])
            pO = Ring([acc[4], acc[5]])
            pJ = Ring([acc[6], acc[7]])
            yvw = self.Y.rearrange("(b j p) n -> b p j n", j=4, p=128)
            for tb_ in range(4):
                ft, yc = ftr.next(), ycr[tb_ % 2]
                for cc in range(2):
                    p = pF.next()
                    S.op("pe", [(lambda h, t=t: h.matmul(p[:], lhsT=ccb[:, t, :], rhs=y12[:, t, cc, tb_ * 512:(tb_ + 1) * 512],
                                                         start=(t == 0), stop=(t == 1))) for t in range(2)], reads=[ccb, y12], writes=[p])
                    S.op(ev(cc), cp(ev(cc), ft[:, cc, :], p[:]), reads=[p], writes=[ft])
                for half in range(2):
                    po = pO.next()
                    S.op("pe", [(lambda h, jj=jj, cc=cc: h.matmul(po[:, jj * 256 + cc * 128:jj * 256 + (cc + 1) * 128],
                                                                  lhsT=ft[:, cc, (half * 2 + jj) * 128:(half * 2 + jj + 1) * 128],
                                                                  rhs=wf[:, cc, :], start=True, stop=True))
                                for jj in range(2) for cc in range(2)], reads=[ft, wf], writes=[po])
                    for jj in range(2):
                        S.op("dve", lambda h, jj=jj: h.tensor_tensor(out=yc[:, half * 2 + jj, :], in0=po[:, jj * 256:(jj + 1) * 256],
                                                                     in1=fnb[:], op=ALU.add), reads=[po, fnb], writes=[yc])
                S.dma("sp", yvw[tb_][:, :, 512:768], yc[:], reads=[yc])
            for tb_ in range(4, 8):
                ft, yc = ftr.next(), ycr[tb_ % 2]
                for j in range(4):
                    lo = NT - (tb_ * 512 + j * 128) - 127
                    for cc in range(2):
                        p = pF.next()
                        S.op("pe", [(lambda h, t=t: h.matmul(p[:, 0:128], lhsT=ccb[:, 2 * t, :], rhs=y12[:, t, cc, lo:lo + 128],
                                                             start=(t == 0), stop=(t == 1))) for t in range(2)], reads=[ccb, y12], writes=[p])
                        S.op(ev(cc), cp(ev(cc), ft[:, cc, j * 128:(j + 1) * 128], p[:, 0:128]), reads=[p], writes=[ft])
                orvs = []
                for half in range(2):
                    po = pO.next()
                    S.op("pe", [(lambda h, jj=jj, cc=cc: h.matmul(po[:, jj * 256 + cc * 128:jj * 256 + (cc + 1) * 128],
                                                                  lhsT=ft[:, cc, (half * 2 + jj) * 128:(half * 2 + jj + 1) * 128],
                                                                  rhs=wf[:, cc, :], start=True, stop=True))
                                for jj in range(2) for cc in range(2)], reads=[ft, wf], writes=[po])
                    for jj in range(2):
                        orv = orr.next()
                        orvs.append(orv)
                        S.op("dve", lambda h, jj=jj: h.tensor_tensor(out=orv[:], in0=po[:, jj * 256:(jj + 1) * 256], in1=fnb[:], op=ALU.add),
                             reads=[po, fnb], writes=[orv])
                for half in range(2):
                    pj = pJ.next()
                    S.op("pe", [(lambda h, jj=jj: h.matmul(pj[:, jj * 256:(jj + 1) * 256], lhsT=ccb[:, 3, :], rhs=orvs[half * 2 + jj][:],
                                                           start=True, stop=True)) for jj in range(2)],
                         reads=[ccb, orvs[half * 2], orvs[half * 2 + 1]], writes=[pj])
                    for jj in range(2):
                        S.op("act", cp("act", yc[:, half * 2 + jj, :], pj[:, jj * 256:(jj + 1) * 256]), reads=[pj], writes=[yc])
                S.dma("sp", yvw[tb_][:, :, 512:768], yc[:], reads=[yc])

    def p5(self, l, xsrc, after_loads=None):
        S = self.S
        NBK = 16
        with Phase(self) as ph:
            wo = ph.sb([128, 8, 1024], BF16, dma="sw")
            for c in range(0, 8, 2):
                S.dma("pool", wo[:, c:c + 2, :], self.w_out[l, c * 128:(c + 2) * 128, :].rearrange("(c p) n -> p c n", p=128), writes=[wo])
            wbg = after_loads() if after_loads is not None else BG(S)
            g2 = ph.sb([128, 1024], F32, dma=True)
            S.dma("sp", g2[:], self.pvec[l:l + 1, O_N2:O_N2 + 1024].partition_broadcast(128), writes=[g2])
            ybr = [ph.sb([128, 2, 1024], BF16, dma=True) for _ in range(2)]
            ytr = [ph.sb([128, 8, 256], BF16) for _ in range(2)]
            xbr = [ph.sb([128, 2, 1024], F32, dma=True) for _ in range(2)]
            xnr = [ph.sb([128, 2, 1024], BF16) for _ in range(2)]
            x2tr = [ph.sb([128, 8, 256], BF16, dma=True) for _ in range(2)]
            junk = ph.sb([128, 1024], BF16)
            ssr = [ph.sb([128, 2], F32) for _ in range(2)]
            rsr = [ph.sb([128, 2], F32) for _ in range(2)]
            ptp = Ring([ph.ps([128, 8, 128], BF16) for _ in range(4)])
            pf = Ring([ph.ps([128, 512]) for _ in range(4)])
            xv = xsrc.rearrange("(b j p) d -> b p j d", j=2, p=128)
            xo = self.X.rearrange("(b j p) d -> b p j d", j=2, p=128)
            yv = self.Y.rearrange("(b j p) d -> b p j d", j=2, p=128)
            x2v = self.XN2T.rearrange("(c p) t -> p c t", p=128)

            def load(b, defer=True):
                f = bg.dma if defer else S.dma
                S.dma("sp", ybr[b % 2][:], yv[b], writes=[ybr[b % 2]])
                f("sp", xbr[b % 2][:], xv[b], writes=[xbr[b % 2]])

            def ytrans(b):
                yb, yT = ybr[b % 2], ytr[b % 2]
                for j in range(2):
                    pt = ptp.next()
                    S.op("pe", [(lambda h, c=c: h.transpose(out=pt[:, c, :], in_=yb[:, j, c * 128:(c + 1) * 128],
                                                            identity=self.idb[:])) for c in range(8)], reads=[yb, self.idb], writes=[pt])
                    S.op(ev(j), cp(ev(j), yT[:, :, j * 128:(j + 1) * 128], pt[:]), reads=[pt], writes=[yT])

            def proj(b, j):
                yT, xb, ss = ytr[b % 2], xbr[b % 2], ssr[b % 2]
                for n in range(2):
                    p = pf.next()
                    S.op("pe", [(lambda h, c=c: h.matmul(p[:], lhsT=yT[:, c, j * 128:(j + 1) * 128], rhs=wo[:, c, n * 512:(n + 1) * 512],
                                                         start=(c == 0), stop=(c == 7))) for c in range(8)], reads=[yT, wo], writes=[p])
                    S.op("dve", lambda h, n=n: h.tensor_tensor(out=xb[:, j, n * 512:(n + 1) * 512], in0=p[:],
                                                               in1=xb[:, j, n * 512:(n + 1) * 512], op=ALU.add), reads=[p, xb], writes=[xb])
                    bg.pump(2)
                S.op("act", lambda h: h.activation(out=junk[:], in_=xb[:, j, :], func=AF.Square, accum_out=ss[:, j:j + 1]),
                     reads=[xb], writes=[junk, ss])

            bg = BG(S)

            def norm(b):
                xb, ss, rstd, xn = xbr[b % 2], ssr[b % 2], rsr[b % 2], xnr[b % 2]
                bg.dma("sp", xo[b], xb[:], reads=[xb])
                bg.op("act", lambda h: h.activation(out=ss[:], in_=ss[:], func=AF.Ln, scale=1.0 / D, bias=self.eps[:]), reads=[ss, self.eps], writes=[ss])
                bg.op("act", lambda h: h.activation(out=rstd[:], in_=ss[:], func=AF.Exp, scale=-0.5), reads=[ss], writes=[rstd])
                for j in range(2):
                    bg.op("dve", lambda h, j=j: h.scalar_tensor_tensor(out=xn[:, j, :], in0=xb[:, j, :], scalar=rstd[:, j:j + 1], in1=g2[:],
                                                                      op0=ALU.mult, op1=ALU.mult), reads=[xb, rstd, g2], writes=[xn])

            def xtrans(b):
                xn, x2t = xnr[b % 2], x2tr[b % 2]
                for j in range(2):
                    pt = ptp.next()
                    S.op("pe", [(lambda h, c=c: h.transpose(out=pt[:, c, :], in_=xn[:, j, c * 128:(c + 1) * 128],
                                                            identity=self.idb[:])) for c in range(8)], reads=[xn, self.idb], writes=[pt])
                    S.op(ev(j), cp(ev(j), x2t[:, :, j * 128:(j + 1) * 128], pt[:]), reads=[pt], writes=[x2t])
                S.dma("sp", x2v[:, :, b * 256:(b + 1) * 256], x2t[:], reads=[x2t])

            load(0, False)
            load(1, False)
            ytrans(0)
            for b in range(NBK):
                wbg.pump(1)
                proj(b, 0)
                proj(b, 1)
                bg.flush()
                if b + 1 < NBK:
                    ytrans(b + 1)
                if b > 0:
                    xtrans(b - 1)
                norm(b)
                if b + 2 < NBK:
                    load(b + 2)
            bg.flush()
            xtrans(NBK - 1)
            wbg.flush()

    def load_wu(self, l, wu):
        bg = BG(self.S)
        for c in range(8):
            for hh in range(2):
                bg.dma("pool", wu[:, c, hh * 2048:(hh + 1) * 2048], self.w_up[l, c * 128:(c + 1) * 128, hh * 2048:(hh + 1) * 2048], writes=[wu])
        return bg

    def load_wd(self, l, wd):
        bg = BG(self.S)
        for c in range(0, 32, 2):
            bg.dma("pool", wd[:, c:c + 2, :], self.w_down[l, c * 128:(c + 2) * 128, :].rearrange("(c p) n -> p c n", p=128), writes=[wd])
        return bg

    def p6(self, l, last, wu, wd):
        S = self.S
        with Phase(self) as ph:
            fg = None
            if last and self.final:
                fg = ph.sb([128, 1024], F32, dma=True)
                S.dma("sp", fg[:], self.pvec[0:1, O_FIN:O_FIN + 1024].partition_broadcast(128), writes=[fg])
                junk = ph.sb([128, 1024], BF16)
                ssr = Ring([ph.sb([128, 1], F32) for _ in range(2)])
            x2r = [ph.sb([128, 8, 512], BF16, dma=True) for _ in range(2)]
            uT = ph.sb([128, 32, 512], BF16)
            rr = Ring([ph.sb([128, 512], F32) for _ in range(2)])
            xtr = Ring([ph.sb([128, 1024], F32, dma=True) for _ in range(3)])
            pu = Ring([ph.ps([128, 512]) for _ in range(4)])
            pd = Ring([ph.ps([128, 512]) for _ in range(4)])
            x2v = self.XN2T.rearrange("(c p) t -> p c t", p=128)
            dst = self.out if (last and self.final) else self.X
            for b in range(8):
                x2 = x2r[b % 2]
                if b == 0:
                    S.dma("sp", x2[:], x2v[:, :, 0:512], writes=[x2])
                if b + 1 < 8:
                    S.dma("sp", x2r[(b + 1) % 2][:], x2v[:, :, (b + 1) * 512:(b + 2) * 512], writes=[x2r[(b + 1) % 2]])
                for f in range(32):
                    p, r = pu.next(), rr.next()
                    S.op("pe", [(lambda h, c=c: h.matmul(p[:], lhsT=wu[:, c, f * 128:(f + 1) * 128], rhs=x2[:, c, :],
                                                         start=(c == 0), stop=(c == 7))) for c in range(8)], reads=[wu, x2], writes=[p])
                    S.op("act", lambda h: h.activation(out=r[:], in_=p[:], func=AF.Relu), reads=[p], writes=[r])
                    S.op("dve" if f % 2 == 0 else "pool", lambda h, f=f: h.tensor_tensor(out=uT[:, f, :], in0=r[:], in1=r[:], op=ALU.mult),
                         reads=[r], writes=[uT])
                for j in range(4):
                    xt = xtr.next()
                    t0 = b * 512 + j * 128
                    S.dma("sp", xt[:], self.X[t0:t0 + 128, :], writes=[xt])
                    for n in range(2):
                        p = pd.next()
                        S.op("pe", [(lambda h, f=f: h.matmul(p[:], lhsT=uT[:, f, j * 128:(j + 1) * 128], rhs=wd[:, f, n * 512:(n + 1) * 512],
                                                             start=(f == 0), stop=(f == 31))) for f in range(32)], reads=[uT, wd], writes=[p])
                        S.op("dve", lambda h, n=n: h.tensor_tensor(out=xt[:, n * 512:(n + 1) * 512], in0=p[:], in1=xt[:, n * 512:(n + 1) * 512],
                                                                   op=ALU.add), reads=[p, xt], writes=[xt])
                    if fg is not None:
                        ss = ssr.next()
                        S.op("act", lambda h: h.activation(out=junk[:], in_=xt[:], func=AF.Square, accum_out=ss[:]), reads=[xt], writes=[junk, ss])
                        S.op("act", lambda h: h.activation(out=ss[:], in_=ss[:], func=AF.Ln, scale=1.0 / D, bias=self.eps[:]), reads=[ss, self.eps], writes=[ss])
                        S.op("act", lambda h: h.activation(out=ss[:], in_=ss[:], func=AF.Exp, scale=-0.5), reads=[ss], writes=[ss])
                        S.op("dve", lambda h: h.scalar_tensor_tensor(out=xt[:], in0=xt[:], scalar=ss[:, 0:1], in1=fg[:], op0=ALU.mult, op1=ALU.mult),
                             reads=[xt, ss, fg], writes=[xt])
                    S.dma("sp", dst[t0:t0 + 128, :], xt[:], reads=[xt])


_CONST = {}


def _consts():
    if _CONST:
        return _CONST
    bf = ml_dtypes.bfloat16
    c = {}
    c["c_identb"] = np.eye(128, dtype=np.float32).astype(bf)
    c["c_identf"] = np.eye(128, dtype=np.float32)
    t = np.arange(NT)
    aug = np.zeros((4, 12, NT), np.float32)
    db = np.zeros((128, 4, 128), np.float32)
    ii = np.arange(128)
    for h in range(4):
        s = 2.0 ** (-8.0 * (h + 1) / 4)
        aug[h, 0] = s * (t % 128)
        aug[h, 1] = s * 128 * (t // 128)
        aug[h, 2] = 1
        aug[h, 3] = 1
        aug[h, 4] = -1
        aug[h, 5] = -1
        aug[h, 6] = s * (t % 128)
        aug[h, 7] = s * 128 * (t // 128)
        aug[h, 8:12] = -aug[h, 4:8]
        db[:, h, :] = -2.0 * s * np.maximum(ii[:, None] - ii[None, :], 0)
    c["c_aug"] = aug.astype(bf)
    c["c_db"] = db.astype(bf)
    ang = (2 * np.pi / NT) * np.arange(NT)
    ctab, stab = np.cos(ang) / 64.0, np.sin(ang) / 64.0
    idx = (np.arange(NT, dtype=np.int64)[:, None] * np.arange(NT, dtype=np.int64)[None, :]) % NT
    cosm = ctab[idx].astype(np.float32).astype(bf)
    sinm = stab[idx].astype(np.float32).astype(bf)
    cs = np.stack([cosm, sinm], axis=0).reshape(2, 16, 2, 128, 4, 1024)
    c["c_tab"] = np.ascontiguousarray(cs.transpose(4, 1, 3, 2, 0, 5)[0:2])
    cc = np.zeros((128, 5, 128), np.float32)
    a64 = (2 * np.pi / 64) * ((np.arange(64)[:, None] * np.arange(64)[None, :]) % 64)
    for g in range(2):
        cc[g * 64:(g + 1) * 64, 0, g * 64:(g + 1) * 64] = np.cos(a64) / 8.0
        cc[g * 64:(g + 1) * 64, 1, g * 64:(g + 1) * 64] = -np.sin(a64) / 8.0
        cc[g * 64:(g + 1) * 64, 2, g * 64:(g + 1) * 64] = np.sin(a64) / 8.0
    cc[np.arange(128), 3, 127 - np.arange(128)] = 1.0
    cc[:, 4, 0] = ((-1.0) ** np.arange(128)) / 64.0
    c["c_cc"] = cc.astype(bf)
    g64 = np.zeros((128, 128), np.float32)
    g64[0:64, 0:64] = 1.0 / 64
    g64[64:128, 64:128] = 1.0 / 64
    c["c_g64"] = g64
    _CONST.update(c)
    return _CONST


def _host_layout(inp):
    f = np.float32
    w_in = np.asarray(inp["w_in"], f)
    perm = np.concatenate([np.arange(0, 256), np.arange(256, 512), np.arange(768, 1024), np.arange(1024, 1280),
                           np.arange(512, 768), np.arange(1280, 1536), np.arange(1536, 1792), np.arange(1792, 2048)])
    m = {}
    m["w_in"] = np.ascontiguousarray(w_in[:, :, perm])
    m["w_out"] = np.ascontiguousarray(inp["w_out"], f)
    m["w_up"] = np.ascontiguousarray(inp["w_up"], f)
    m["w_down"] = np.ascontiguousarray(inp["w_down"], f)
    pv = np.zeros((NL, NV), f)
    pv[:, O_N1:O_N1 + 1024] = inp["norm1_g"]
    pv[:, O_N2:O_N2 + 1024] = inp["norm2_g"]
    pv[:, O_SLG:O_SLG + 256] = inp["sgu_ln_g"]
    pv[:, O_SLB:O_SLB + 256] = inp["sgu_ln_b"]
    pv[:, O_PWB:O_PWB + 256] = inp["conv_pw_b"]
    pv[:, O_FNB:O_FNB + 256] = inp["fnet_b"]
    pv[:, O_SUB:O_SUB + 64] = inp["subln_g"]
    pv[:, O_LAM:O_LAM + 32] = inp["lam_q1"]
    pv[:, O_LAM + 32:O_LAM + 64] = inp["lam_k1"]
    pv[:, O_LAM + 64:O_LAM + 96] = inp["lam_q2"]
    pv[:, O_LAM + 96:O_LAM + 128] = inp["lam_k2"]
    pv[:, O_FIN:O_FIN + 1024] = np.asarray(inp["final_g"], f)[None, :]
    m["pvec"] = pv
    cv = np.zeros((NL, 128, NCV), f)
    cv[:, :, 0:2] = np.asarray(inp["conv_dw_b"], f).reshape(NL, 2, 128).transpose(0, 2, 1)
    cv[:, :, 2:4] = np.asarray(inp["conv_ln_g"], f).reshape(NL, 2, 128).transpose(0, 2, 1)
    cv[:, :, 4:6] = np.asarray(inp["conv_ln_b"], f).reshape(NL, 2, 128).transpose(0, 2, 1)
    cv[:, :, 6:10] = np.asarray(inp["sgu_b"], f).transpose(0, 2, 1)
    m["colv"] = cv
    dw = np.asarray(inp["conv_dw_w"], f)
    cd = np.zeros((NL, 128, 62, 128), f)
    pp = np.arange(128)
    for c in range(2):
        for j in range(31):
            cd[:, pp, c * 31 + j, pp] = dw[:, j, c * 128:(c + 1) * 128]
    m["convdiag"] = cd
    m["sgu_wT"] = np.ascontiguousarray(np.asarray(inp["sgu_w"], f).transpose(0, 3, 1, 2))
    fw = np.asarray(inp["fnet_w"], f)
    fb = np.zeros((NL, 128, 2, 128), f)
    for g in range(4):
        c, gg = g // 2, g % 2
        fb[:, gg * 64:(gg + 1) * 64, c, gg * 64:(gg + 1) * 64] = fw[:, g]
    m["fnet_blk"] = fb
    m["pw_w"] = np.ascontiguousarray(inp["conv_pw_w"], f)
    m.update(_consts())
    return m


_NC_CACHE = {}


def kernel(**inputs):
    x = np.ascontiguousarray(np.asarray(inputs["x"], np.float32))
    shared = _host_layout(inputs)
    if "nc" not in _NC_CACHE:
        _NC_CACHE["nc"] = MK().build()
    nc = _NC_CACHE["nc"]
    in_maps = []
    for b in range(8):
        m = dict(shared)
        m["x"] = x[b]
        in_maps.append(m)
    res = run_bass_kernel_spmd(nc, in_maps, core_ids=list(range(8)))
    return np.stack([np.asarray(r["out"], np.float32) for r in res.results], axis=0)
```
